# Optimizing a Trainium2 kernel written in Bass

```python
import jax, jax.numpy as jnp
from jax import lax
import numpy as np

D_MODEL = 4096
BATCH = 2
SEQ = 8192
DEPTH = 4

N_MIXERS = 3
ATTN_HEAD_DIM = 64
ATTN_Q_HEADS = D_MODEL // ATTN_HEAD_DIM
ATTN_KV_HEADS = 8
ATTN_GROUP = ATTN_Q_HEADS // ATTN_KV_HEADS
ATTN_QKV_DIM = (ATTN_Q_HEADS + 2 * ATTN_KV_HEADS) * ATTN_HEAD_DIM
WINDOW = 128
ATTN_BLOCK = WINDOW
ROPE_THETA = 10000.0
HGRN_EXPAND = 128
HGRN_HEADS = D_MODEL // HGRN_EXPAND
HGRN_DK = HGRN_EXPAND
HGRN_DV = D_MODEL // HGRN_HEADS
HGRN_CHUNK = 64
CONV_WIDTH = 3
D_FF = 11008
MLP_CONV_WIDTH = 3
NORM_EPS = 1e-6

LAYER_KINDS = tuple(i % N_MIXERS for i in range(DEPTH))
N_ATTN = LAYER_KINDS.count(0)
N_HGRN = LAYER_KINDS.count(1)
N_CONV = LAYER_KINDS.count(2)

kernel_name = 'hybrid_swa_hgrn2_shortconv_convglu'


def rms_norm(x, gain):
    xf = x.astype(jnp.float32)
    y = xf * lax.rsqrt(jnp.mean(xf * xf, axis=-1, keepdims=True) + NORM_EPS)
    return (y * gain.astype(jnp.float32)).astype(x.dtype)


def causal_depthwise_conv(x, w, b=None):
    width = w.shape[0]
    S = x.shape[1]
    xp = jnp.pad(x, ((0, 0), (width - 1, 0), (0, 0)))
    y = w[0] * xp[:, 0:S]
    for j in range(1, width):
        y = y + w[j] * xp[:, j:j + S]
    if b is not None:
        y = y + b
    return y


def rope(x, positions):
    half = x.shape[-1] // 2
    inv_freq = ROPE_THETA ** (-jnp.arange(half, dtype=jnp.float32) / half)
    ang = positions.astype(jnp.float32)[..., None] * inv_freq
    cos = jnp.cos(ang)[:, :, None, :]
    sin = jnp.sin(ang)[:, :, None, :]
    xf = x.astype(jnp.float32)
    x1, x2 = xf[..., :half], xf[..., half:]
    out = jnp.concatenate([x1 * cos - x2 * sin, x2 * cos + x1 * sin], axis=-1)
    return out.astype(x.dtype)


def sliding_window_attention(h, positions, w_qkv, b_qkv, q_gain, k_gain, sinks, w_o, b_o):
    Bt, S, _ = h.shape
    hd, Hq, Hkv, G, BLK = ATTN_HEAD_DIM, ATTN_Q_HEADS, ATTN_KV_HEADS, ATTN_GROUP, ATTN_BLOCK
    nb = S // BLK
    qkv = h @ w_qkv + b_qkv
    q, k, v = jnp.split(qkv, [Hq * hd, (Hq + Hkv) * hd], axis=-1)
    q = rope(rms_norm(q.reshape(Bt, S, Hq, hd), q_gain), positions)
    k = rope(rms_norm(k.reshape(Bt, S, Hkv, hd), k_gain), positions)
    v = v.reshape(Bt, S, Hkv, hd)
    q_blocks = q.reshape(Bt, nb, BLK, Hkv, G, hd).transpose(1, 0, 3, 4, 2, 5)

    def band(t):
        tp = jnp.pad(t, ((0, 0), (BLK, 0), (0, 0), (0, 0))).reshape(Bt, nb + 1, BLK, Hkv, hd)
        return jnp.concatenate([tp[:, :-1], tp[:, 1:]], axis=2).transpose(1, 0, 3, 2, 4)

    k_band, v_band = band(k), band(v)
    q_idx = jnp.arange(BLK)[:, None] + BLK
    k_idx = jnp.arange(2 * BLK)[None, :]
    rel = q_idx - k_idx
    band_mask = (rel >= 0) & (rel < WINDOW)
    sink = sinks.astype(jnp.float32).reshape(Hkv, G)[None, :, :, None, None]
    scale = ATTN_HEAD_DIM ** -0.5

    def attend_block(args):
        qb, kb, vb, blk = args
        s = jnp.einsum('bhgqd,bhkd->bhgqk', qb, kb).astype(jnp.float32) * scale
        valid = band_mask & ((blk - 1) * BLK + k_idx >= 0)
        s = jnp.where(valid, s, -jnp.inf)
        m = jnp.maximum(jnp.max(s, axis=-1, keepdims=True), sink)
        p = jnp.exp(s - m)
        denom = jnp.sum(p, axis=-1, keepdims=True) + jnp.exp(sink - m)
        return jnp.einsum('bhgqk,bhkd->bhgqd', (p / denom).astype(vb.dtype), vb)

    o = lax.map(attend_block, (q_blocks, k_band, v_band, jnp.arange(nb)))
    o = o.transpose(1, 0, 4, 2, 3, 5).reshape(Bt, S, Hq * hd)
    return o @ w_o + b_o


def hgrn2_mixer(h, lower_bound, w_in, out_gain, w_o):
    Bt, S, _ = h.shape
    H, DK, DV, C = HGRN_HEADS, HGRN_DK, HGRN_DV, HGRN_CHUNK
    nc = S // C
    qfig = h @ w_in
    q, f, i, g = jnp.split(qfig, 4, axis=-1)
    q = jax.nn.silu(q.astype(jnp.float32))
    ff = f.astype(jnp.float32)
    lb = lower_bound.astype(jnp.float32)
    log_forget = jnp.log(lb + (1.0 - lb) * jax.nn.sigmoid(ff))
    k = (1.0 - lb) * jax.nn.sigmoid(-ff)

    def to_chunks(t, d):
        return t.reshape(Bt, nc, C, H, d).transpose(1, 0, 3, 2, 4)

    qc, kc, gc = to_chunks(q, DK), to_chunks(k, DK), to_chunks(log_forget, DK)
    vc = to_chunks(i.astype(jnp.float32), DV)
    causal = jnp.tril(jnp.ones((C, C), dtype=bool))[:, :, None]

    def chunk_step(state, inp):
        qb, kb, vb, gb = inp
        b = jnp.cumsum(gb, axis=2)
        diff = b[:, :, :, None, :] - b[:, :, None, :, :]
        decay = jnp.exp(jnp.where(causal, diff, -jnp.inf))
        scores = jnp.einsum('bhtk,bhsk,bhtsk->bhts', qb, kb, decay)
        o = jnp.einsum('bhts,bhsv->bhtv', scores, vb) + jnp.einsum('bhtk,bhkv->bhtv', qb * jnp.exp(b), state)
        b_last = b[:, :, -1:, :]
        state = jnp.exp(b_last[:, :, 0, :])[..., None] * state + jnp.einsum('bhsk,bhsv->bhkv', kb * jnp.exp(b_last - b), vb)
        return state, o

    state0 = jnp.zeros((Bt, H, DK, DV), jnp.float32)
    _, o = lax.scan(chunk_step, state0, (qc, kc, vc, gc))
    o = o.transpose(1, 0, 3, 2, 4).reshape(Bt, S, H, DV).astype(h.dtype)
    o = rms_norm(o, out_gain) * jax.nn.silu(g.reshape(Bt, S, H, DV))
    return o.reshape(Bt, S, H * DV) @ w_o


def short_conv_mixer(h, w_in, conv_w, w_out):
    bcu = h @ w_in
    b_gate, c_gate, u = jnp.split(bcu, 3, axis=-1)
    y = b_gate * causal_depthwise_conv(c_gate * u, conv_w)
    return y @ w_out


def conv_glu(h, w_up, conv_w, conv_b, w_down):
    gu = h @ w_up
    g, u = jnp.split(gu, 2, axis=-1)
    g = causal_depthwise_conv(g, conv_w, conv_b)
    return (jax.nn.silu(g) * u) @ w_down


def setup_inputs(seed: int = 0) -> dict:
    key = jax.random.key(seed)
    ks = jax.random.split(key, 24)
    D = D_MODEL
    nrm = jax.random.normal
    f32 = jnp.float32
    x = nrm(ks[0], (BATCH, SEQ, D), f32)
    offset = jax.random.randint(ks[1], (BATCH, 1), 0, 1024, dtype=jnp.int32)
    positions = offset + jnp.arange(SEQ, dtype=jnp.int32)[None, :]
    return {
        'x': x,
        'positions': positions,
        'mixer_norm': 1.0 + 0.02 * nrm(ks[2], (DEPTH, D), f32),
        'mlp_norm': 1.0 + 0.02 * nrm(ks[3], (DEPTH, D), f32),
        'attn_w_qkv': nrm(ks[4], (N_ATTN, D, ATTN_QKV_DIM), f32) * D ** -0.5,
        'attn_b_qkv': 0.02 * nrm(ks[5], (N_ATTN, ATTN_QKV_DIM), f32),
        'attn_q_norm': 1.0 + 0.02 * nrm(ks[6], (N_ATTN, ATTN_HEAD_DIM), f32),
        'attn_k_norm': 1.0 + 0.02 * nrm(ks[7], (N_ATTN, ATTN_HEAD_DIM), f32),
        'attn_sinks': 0.5 * nrm(ks[8], (N_ATTN, ATTN_Q_HEADS), f32),
        'attn_w_o': nrm(ks[9], (N_ATTN, D, D), f32) * D ** -0.5,
        'attn_b_o': 0.02 * nrm(ks[10], (N_ATTN, D), f32),
        'hgrn_lower_bounds': 0.1 * nrm(ks[11], (DEPTH, D), f32),
        'hgrn_w_in': nrm(ks[12], (N_HGRN, D, 4 * D), f32) * D ** -0.5,
        'hgrn_out_norm': 1.0 + 0.02 * nrm(ks[13], (N_HGRN, HGRN_DV), f32),
        'hgrn_w_o': nrm(ks[14], (N_HGRN, D, D), f32) * D ** -0.5,
        'conv_w_in': nrm(ks[15], (N_CONV, D, 3 * D), f32) * D ** -0.5,
        'conv_w': nrm(ks[16], (N_CONV, CONV_WIDTH, D), f32) * CONV_WIDTH ** -0.5,
        'conv_w_out': nrm(ks[17], (N_CONV, D, D), f32) * D ** -0.5,
        'mlp_w_up': nrm(ks[18], (DEPTH, D, 2 * D_FF), f32) * D ** -0.5,
        'mlp_conv_w': nrm(ks[19], (DEPTH, MLP_CONV_WIDTH, D_FF), f32) * MLP_CONV_WIDTH ** -0.5,
        'mlp_conv_b': 0.02 * nrm(ks[20], (DEPTH, D_FF), f32),
        'mlp_w_down': nrm(ks[21], (DEPTH, D_FF, D), f32) * D_FF ** -0.5,
    }


def reference(x, positions, mixer_norm, mlp_norm, attn_w_qkv, attn_b_qkv, attn_q_norm, attn_k_norm, attn_sinks, attn_w_o, attn_b_o, hgrn_lower_bounds, hgrn_w_in, hgrn_out_norm, hgrn_w_o, conv_w_in, conv_w, conv_w_out, mlp_w_up, mlp_conv_w, mlp_conv_b, mlp_w_down):
    lb_soft = jax.nn.softmax(hgrn_lower_bounds.astype(jnp.float32), axis=0)
    lb_table = jnp.cumsum(lb_soft, axis=0) - lb_soft[0:1]
    for layer in range(DEPTH):
        kind = LAYER_KINDS[layer]
        j = LAYER_KINDS[:layer].count(kind)
        h = rms_norm(x, mixer_norm[layer])
        if kind == 0:
            mix = sliding_window_attention(h, positions, attn_w_qkv[j], attn_b_qkv[j], attn_q_norm[j], attn_k_norm[j], attn_sinks[j], attn_w_o[j], attn_b_o[j])
        elif kind == 1:
            mix = hgrn2_mixer(h, lb_table[layer], hgrn_w_in[j], hgrn_out_norm[j], hgrn_w_o[j])
        else:
            mix = short_conv_mixer(h, conv_w_in[j], conv_w[j], conv_w_out[j])
        x = x + mix.astype(x.dtype)
        x = x + conv_glu(rms_norm(x, mlp_norm[layer]), mlp_w_up[layer], mlp_conv_w[layer], mlp_conv_b[layer], mlp_w_down[layer]).astype(x.dtype)
    return x
```

```python
import math
import os
from contextlib import ExitStack

import numpy as np
import ml_dtypes
import concourse.bass as bass
import concourse.mybir as mybir
from concourse.bass_utils import run_bass_kernel_spmd

F32, BF16, I32 = mybir.dt.float32, mybir.dt.bfloat16, mybir.dt.int32
AF = mybir.ActivationFunctionType
ALU = mybir.AluOpType
P = 128
EPS = 1e-6


class Buf:
    __slots__ = ("name", "w", "r")

    def __init__(self, name):
        self.name = name
        self.w = None
        self.r = {}


class Prog:
    ENG = ("pe", "act", "dve", "pool", "sp")

    def __init__(self):
        self.ops = {e: [] for e in self.ENG}
        self.cnt = {}
        self.waited = {e: {} for e in self.ENG}

    def op(self, eng, fn, reads=(), writes=(), dma=None, relax=False):
        deps = {}

        def add(t):
            if t is not None and deps.get(t[0], 0) < t[1]:
                deps[t[0]] = t[1]

        for b in reads:
            add(b.w)
        for b in writes:
            add(b.w)
            for k, v in b.r.items():
                add((k, v))
        key = eng if dma is None else dma
        inc = 1 if dma is None else 16
        if relax:
            deps.pop(key, None)
        self.cnt[key] = self.cnt.get(key, 0) + inc
        tok = (key, self.cnt[key])
        wd = self.waited[eng]
        waits = []
        for k, v in deps.items():
            if wd.get(k, 0) < v:
                wd[k] = v
                waits.append((k, v))
        self.ops[eng].append((waits, fn, key, inc))
        for b in reads:
            if b.r.get(key, 0) < tok[1]:
                b.r[key] = tok[1]
        for b in writes:
            b.w = tok
            b.r = {}
        return tok

    def finish(self, eng):
        waits = [(k, v) for k, v in self.cnt.items() if self.waited[eng].get(k, 0) < v]
        self.ops[eng].append((waits, None, None, 0))

    def emit(self, nc):
        keys = sorted(self.cnt.keys())
        with ExitStack() as st:
            sems = {k: st.enter_context(nc.semaphore("s_" + k)) for k in keys}
            block = st.enter_context(nc.Block())

            def run(name, e):
                for waits, fn, key, inc in self.ops[name]:
                    for k, v in waits:
                        e.wait_ge(sems[k], v)
                    if fn is not None:
                        fn(e).then_inc(sems[key], inc)

            @block.tensor
            def _(e):
                run("pe", e)

            @block.scalar
            def _(e):
                run("act", e)

            @block.vector
            def _(e):
                run("dve", e)

            @block.gpsimd
            def _(e):
                run("pool", e)

            @block.sync
            def _(e):
                run("sp", e)


class Cfg:
    def __init__(self, D=4096, DFF=11008, S=8192, kinds=(0, 1, 2, 0), HKV=8, NT=512):
        self.D, self.DFF, self.S, self.kinds, self.HKV, self.NT = D, DFF, S, tuple(kinds), HKV, NT
        self.KC = D // P
        self.FC = DFF // P
        self.FH = (self.FC + 1) // 2
        self.HQ = D // 64
        self.G = self.HQ // HKV
        self.CPG = self.G // 2
        self.VG = min(8, self.KC)
        self.NKG = self.KC // self.VG
        self.L = len(kinds)
        self.NTI = S // NT
        self.WSLOT = max(self.KC * P, self.FH * P, self.VG * HKV * 64)
        self.NG = 3
        self.col = {"invf": 0, "eps": 1, "hpi": 2}
        self.LMAX = 0
        for l, k in enumerate(kinds):
            c = self.NG
            ws = [("mn", self.KC), ("fn", self.KC), ("mcw", 3 * self.FC), ("mcb", self.FC)]
            if k == 0:
                ws += [("bqk", self.KC + HKV), ("qg", 1), ("kg", 1), ("snk", self.KC), ("bo", self.KC)]
            elif k == 1:
                ws += [("lbs", self.L * self.KC), ("og", 1)]
            else:
                ws += [("cw", 3 * self.KC)]
            for nm, w in ws:
                self.col[(l, nm)] = c
                c += w
            self.LMAX = max(self.LMAX, c - self.NG)
        self.NSM = self.NG + self.LMAX
        self.NSMD = self.NG + self.L * self.LMAX

    def weights(self, l):
        k = self.kinds[l]
        KC, FC, FH = self.KC, self.FC, self.FH
        if k == 0:
            ws = [("wq", KC, KC * P), ("wk", self.HKV, KC * P), ("wv", self.NKG, self.VG * self.HKV * 64), ("wo", KC, KC * P)]
        elif k == 1:
            ws = [("win", 4 * KC, KC * P), ("wo", KC, KC * P)]
        else:
            ws = [("win", 3 * KC, KC * P), ("wo", KC, KC * P)]
        ws += [("wup", 2 * FC, KC * P), ("wdn", 2 * KC, FH * P)]
        return [("L%d_%s" % (l, n), nb, e) for n, nb, e in ws]


def build_nc(cfg):
    HGSTOP = int(os.environ.get('HG_STOP', '99'))
    HGSUB = int(os.environ.get('HG_SUB', '99'))
    D, S, NT, KC, FC, FH, HKV, CPG = cfg.D, cfg.S, cfg.NT, cfg.KC, cfg.FC, cfg.FH, cfg.HKV, cfg.CPG
    NTI, L = cfg.NTI, cfg.L
    NB = NT // P
    nc = bass.Bass("TRN2", target_bir_lowering=False)
    pr = Prog()
    xin = nc.dram_tensor("xT", [D, S], F32, kind="ExternalInput").ap()
    yout = nc.dram_tensor("yT", [D, S], F32, kind="ExternalOutput").ap()
    posr = nc.dram_tensor("posr", [P, S], I32, kind="ExternalInput").ap()
    smd = nc.dram_tensor("sm", [P, cfg.NSMD], F32, kind="ExternalInput").ap()
    cf32 = nc.dram_tensor("cf32", [P, 3 * P], F32, kind="ExternalInput").ap()
    cbf = nc.dram_tensor("cbf", [P, 4 * NT + P + 64], F32, kind="ExternalInput").ap()
    bvd = nc.dram_tensor("bvrep", [P, max(1, cfg.kinds.count(0)) * HKV * 64], F32, kind="ExternalInput").ap()
    wext, wbf = {}, {}
    for l in range(L):
        for name, nb, e in cfg.weights(l):
            wext[name] = nc.dram_tensor(name, [nb, P, e], F32, kind="ExternalInput").ap()
            wbf[name] = nc.dram_tensor(name + "_b", [nb, P, e], BF16).ap()
    xm = nc.dram_tensor("xm", [D, S], F32).ap()
    xn = nc.dram_tensor("xn", [D, S], F32).ap()
    cosd = nc.dram_tensor("cosd", [P, S], F32).ap()
    sind = nc.dram_tensor("sind", [P, S], F32).ap()

    st = ExitStack()
    BIGB = max(FC * NT * 2, KC * NT * 4, 90112)

    def sb(name, shape, dt):
        return st.enter_context(nc.sbuf_tensor(name, shape, dt))

    big = sb("big", [P, BIGB // 2], BF16)
    hT = sb("hT", [P, KC, NT], BF16)
    NWS = 5
    wsl = [sb("ws%d" % i, [P, cfg.WSLOT], BF16) for i in range(NWS)]
    NSCR = 5
    scr = [sb("scr%d" % i, [P, NT + 2], F32) for i in range(NSCR)]
    sm = sb("smt", [P, cfg.NSM], F32)
    c32 = sb("c32", [P, 3 * P], F32)
    cb = sb("cbt", [P, 4 * NT + P + 64], BF16)
    cbstage = scr
    psum = [st.enter_context(nc.psum_tensor("ps%d" % i, [P, NT], F32)) for i in range(8)]

    B = {}

    def buf(name):
        if name not in B:
            B[name] = Buf(name)
        return B[name]

    epsc = sm[:, cfg.col["eps"]:cfg.col["eps"] + 1]
    ones32 = c32[:, 0:P]
    bd64 = c32[:, P:2 * P]
    rotm = c32[:, 2 * P:3 * P]
    maskc = cb[:, 0:NT]
    maskp = cb[:, NT:2 * NT]
    hmask = cb[:, 2 * NT:3 * NT]
    resetm = cb[:, 3 * NT:4 * NT]
    ident = cb[:, 4 * NT:4 * NT + P]
    ones64 = cb[:, 4 * NT + P:4 * NT + P + 64]

    ps_i = [0]

    def newps():
        i = ps_i[0] % 8
        ps_i[0] += 1
        return psum[i], buf("ps%d" % i)

    scr_i = [0]

    def newscr():
        i = scr_i[0] % NSCR
        scr_i[0] += 1
        return scr[i], buf("scr%d" % i)

    ws_i = [0]

    def E(eng, fn, reads=(), writes=(), **kw):
        return pr.op(eng, fn, reads=reads, writes=writes, **kw)

    def dma_in(out_ap, in_ap, obuf, ibufs=(), key=None, eng="sp"):
        E(eng, lambda e: e.dma_start(out=out_ap, in_=in_ap), reads=ibufs, writes=(obuf,), dma=key or ("d_" + obuf.name))

    def act(out, in_, func, obuf, ibufs, bias=None, scale=None):
        kw = {}
        if bias is not None:
            kw["bias"] = bias
        if scale is not None:
            kw["scale"] = scale
        E("act", lambda e: e.activation(out=out, in_=in_, func=func, **kw), reads=ibufs, writes=(obuf,))

    def tt(out, a, b, op, obuf, ibufs, eng="dve"):
        E(eng, lambda e: e.tensor_tensor(out=out, in0=a, in1=b, op=op), reads=ibufs, writes=(obuf,))

    def ts(out, a, s1, s2, op0, op1, obuf, ibufs, eng="dve"):
        if s2 is None:
            E(eng, lambda e: e.tensor_scalar(out=out, in0=a, scalar1=s1, scalar2=None, op0=op0), reads=ibufs, writes=(obuf,))
        else:
            E(eng, lambda e: e.tensor_scalar(out=out, in0=a, scalar1=s1, scalar2=s2, op0=op0, op1=op1), reads=ibufs, writes=(obuf,))

    def stt(out, a, s, b, op0, op1, obuf, ibufs):
        E("dve", lambda e: e.scalar_tensor_tensor(out=out, in0=a, scalar=s, in1=b, op0=op0, op1=op1), reads=ibufs, writes=(obuf,))

    def cp(out, in_, obuf, ibufs, eng="dve"):
        E(eng, lambda e: e.tensor_copy(out=out, in_=in_), reads=ibufs, writes=(obuf,))

    def mm(ps_ap, psb, pairs, ibufs, start=True, stop=True, relax=False):
        def fn(e):
            n = len(pairs)
            ins = None
            for i, (o, l, r) in enumerate(pairs):
                ins = e.matmul(o if o is not None else ps_ap, l, r, start=(start and i == 0), stop=(stop and i == n - 1))
            return ins
        E("pe", fn, reads=ibufs, writes=(psb,), relax=relax)

    def mm_groups(psb, groups, ibufs):
        def fn(e):
            ins = None
            for g in groups:
                n = len(g)
                for i, (o, l, r) in enumerate(g):
                    ins = e.matmul(o, l, r, start=(i == 0), stop=(i == n - 1))
            return ins
        E("pe", fn, reads=ibufs, writes=(psb,))

    def rsqrt(out, in_, scale, obuf, ibuf):
        act(out, in_, AF.Sqrt, obuf, (ibuf, buf("sm")), bias=epsc, scale=scale)
        E("dve", lambda e: e.reciprocal(out=out, in_=out), reads=(obuf,), writes=(obuf,))

    def smc(key, off=0, w=1):
        c = cfg.col[key] + off
        return sm[:, c:c + w]

    def run_items(items, depth=None):
        depth = NWS - 1
        q = {}
        nxt = [0]

        def issue(i):
            wname, blk, _ = items[i]
            s = ws_i[0] % NWS
            ws_i[0] += 1
            nel = wbf[wname].shape[2]
            dma_in(wsl[s][:, 0:nel], wbf[wname][blk], buf("ws%d" % s), ibufs=(buf("W_" + wname),), key="w%d" % s)
            q[i] = (wsl[s], buf("ws%d" % s))

        for i in range(len(items)):
            while nxt[0] < len(items) and nxt[0] <= i + depth:
                issue(nxt[0])
                nxt[0] += 1
            sl, sbuf_ = q.pop(i)
            items[i][2](sl, sbuf_)

    dma_in(sm[:, 0:cfg.NG], smd[:, 0:cfg.NG], buf("sm"))
    dma_in(c32[:, :], cf32[:, :], buf("c32"))
    dma_in(cb[:, :], cbf[:, :], buf("cb"), eng="pool")
    KB = (buf("sm"), buf("c32"), buf("cb"))

    def prep_layer(l):
        for name, nb, e in cfg.weights(l):
            per = max(1, (8 << 20) // (P * e * 4))
            b0 = 0
            while b0 < nb:
                b1 = min(nb, b0 + per)
                o, i_ = wbf[name][b0:b1], wext[name][b0:b1]
                E("pool", lambda e_, o=o, i_=i_: e_.dma_start(out=o, in_=i_), writes=(buf("W_" + name),), dma="p_" + name)
                b0 = b1

    def bigv(off, nbytes, dt, pat=None, **kw):
        a = big[:, off // 2:(off + nbytes) // 2]
        if dt == F32:
            a = a.bitcast(F32)
        if pat:
            a = a.rearrange(pat, **kw)
        return a

    xt = bigv(0, KC * NT * 4, F32, "p (c n) -> p c n", n=NT)
    BIG = buf("big")

    def xtile(ap, j):
        return ap.rearrange("(c p) s -> p c s", p=P)[:, :, j * NT:(j + 1) * NT]

    def norm_tile(src_ap, src_buf, gain_key):
        step = max(1, KC // 4)
        for c0 in range(0, KC, step):
            dma_in(xt[:, c0:c0 + step, :], src_ap[:, c0:c0 + step, :], BIG, ibufs=(src_buf,), key="xt")
        psS, psSb = newps()
        for kc in range(KC):
            sq, sqb = newscr()
            act(sq[:, 0:NT], xt[:, kc, :], AF.Square, sqb, (BIG,))
            mm(psS[:, :], psSb, [(None, ones32, sq[:, 0:NT])], (sqb, buf("c32")), start=(kc == 0), stop=(kc == KC - 1), relax=(kc > 0))
        rs, rsb = newscr()
        rsqrt(rs[:, 0:NT], psS[:, :], 1.0 / D, rsb, psSb)
        HT = buf("hT")
        for kc in range(KC):
            stt(hT[:, kc, :], xt[:, kc, :], smc(gain_key, kc), rs[:, 0:NT], ALU.mult, ALU.mult, HT, (BIG, rsb, buf("sm")))

    NXO = 2
    XC = [sb("xc%d" % i, [P, NT], F32) for i in range(NXO)]
    OC = [sb("oc%d" % i, [P, NT], F32) for i in range(NXO)]
    xo_i = [0]

    def residual_store(ps_ap, psb, dc, j, src_ap, src_buf, dst_ap, dst_buf, bias=None):
        i = xo_i[0] % NXO
        xo_i[0] += 1
        xcb, ocb = buf("xc%d" % i), buf("oc%d" % i)
        rows = slice(dc * P, (dc + 1) * P)
        cols = slice(j * NT, (j + 1) * NT)
        dma_in(XC[i][:, :], src_ap[rows, cols], xcb, ibufs=(src_buf,), key="xc%d" % i)
        if bias is None:
            tt(OC[i][:, :], ps_ap, XC[i][:, :], ALU.add, ocb, (psb, xcb))
        else:
            stt(OC[i][:, :], ps_ap, bias, XC[i][:, :], ALU.add, ALU.add, ocb, (psb, xcb, buf("sm")))
        E("pool", lambda e: e.dma_start(out=dst_ap[rows, cols], in_=OC[i][:, :]), reads=(ocb,), writes=(dst_buf,), dma="oc%d" % i)

    def out_proj(l, wname, srcT, srcbuf, j, src_ap, src_bufs, dst_ap, dst_bufs, bias_key=None):
        items = []
        for dc in range(KC):
            def fn(sl, slb, dc=dc):
                ps, psb = newps()
                w = sl[:, 0:KC * P].rearrange("p (k m) -> p k m", m=P)
                mm(ps[:, :], psb, [(None, w[:, kc, :], srcT[:, kc, :]) for kc in range(KC)], (slb, srcbuf))
                residual_store(ps[:, :], psb, dc, j, src_ap, src_bufs[j], dst_ap, dst_bufs[j],
                               bias=None if bias_key is None else smc((l, bias_key), dc))
            items.append(("L%d_%s" % (l, wname), dc, fn))
        run_items(items)

    def conv3(cext, cextb, carry, carryb, idx, wkey, l, nch, bias_key, out, outb, j):
        cp(cext[:, 0:2], carry[:, idx, :], cextb, (carryb,), eng="pool")
        cp(carry[:, idx, :], cext[:, NT:NT + 2], carryb, (cextb,), eng="pool")
        w = lambda k: smc((l, wkey), k * nch + idx)
        if bias_key is None:
            ts(out, cext[:, 2:2 + NT], w(2), None, ALU.mult, None, outb, (cextb, buf("sm")))
        else:
            ts(out, cext[:, 2:2 + NT], w(2), smc((l, bias_key), idx), ALU.mult, ALU.add, outb, (cextb, buf("sm")))
        stt(out, cext[:, 1:1 + NT], w(1), out, ALU.mult, ALU.add, outb, (cextb, outb, buf("sm")))
        stt(out, cext[:, 0:NT], w(0), out, ALU.mult, ALU.add, outb, (cextb, outb, buf("sm")))

    carryF = sb("carryF", [P, FC, 2], F32)
    carryC = sb("carryC", [P, KC, 2], F32)

    def zero(ap, b, eng="pool"):
        E(eng, lambda e: e.memset(ap, 0.0), writes=(b,))

    def ffn_phase(l, src_ap, src_bufs, dst_ap, dst_bufs):
        aT = bigv(0, FC * NT * 2, BF16, "p (c n) -> p c n", n=NT)
        CF = buf("carryF")
        zero(carryF[:, :, :], CF)
        wn_up, wn_dn = "L%d_wup" % l, "L%d_wdn" % l
        for j in range(NTI):
            norm_tile(xtile(src_ap, j), src_bufs[j], (l, "fn"))
            items = []
            state = {}
            for fc in range(FC):
                def fg(sl, slb, fc=fc):
                    ps, psb = newps()
                    w = sl[:, 0:KC * P].rearrange("p (k m) -> p k m", m=P)
                    mm(ps[:, :], psb, [(None, w[:, kc, :], hT[:, kc, :]) for kc in range(KC)], (slb, buf("hT")))
                    state["g"] = (ps, psb)

                def fu(sl, slb, fc=fc):
                    psu, psub = newps()
                    w = sl[:, 0:KC * P].rearrange("p (k m) -> p k m", m=P)
                    mm(psu[:, :], psub, [(None, w[:, kc, :], hT[:, kc, :]) for kc in range(KC)], (slb, buf("hT")))
                    psg, psgb = state["g"]
                    ge, geb = newscr()
                    act(ge[:, 2:2 + NT], psg[:, :], AF.Copy, geb, (psgb,))
                    t, tb = newscr()
                    conv3(ge, geb, carryF, CF, fc, "mcw", l, FC, "mcb", t[:, 0:NT], tb, j)
                    act(t[:, 0:NT], t[:, 0:NT], AF.Silu, tb, (tb,))
                    tt(aT[:, fc, :], t[:, 0:NT], psu[:, :], ALU.mult, BIG, (tb, psub))
                items.append((wn_up, fc, fg))
                items.append((wn_up, FC + fc, fu))
            run_items(items)
            items = []
            for dc in range(KC):
                def f0(sl, slb, dc=dc):
                    ps, psb = newps()
                    state["o"] = (ps, psb)
                    w = sl[:, 0:FH * P].rearrange("p (k m) -> p k m", m=P)
                    mm(ps[:, :], psb, [(None, w[:, k, :], aT[:, k, :]) for k in range(FH)], (slb, BIG), start=True, stop=False)

                def f1(sl, slb, dc=dc):
                    ps, psb = state["o"]
                    w = sl[:, 0:FH * P].rearrange("p (k m) -> p k m", m=P)
                    n2 = FC - FH
                    mm(ps[:, :], psb, [(None, w[:, k, :], aT[:, FH + k, :]) for k in range(n2)], (slb, BIG), start=False, stop=True, relax=True)
                    residual_store(ps[:, :], psb, dc, j, src_ap, src_bufs[j], dst_ap, dst_bufs[j])
                items.append((wn_dn, 2 * dc, f0))
                items.append((wn_dn, 2 * dc + 1, f1))
            run_items(items)

    def conv_phase(l, src_ap, src_bufs, dst_ap, dst_bufs):
        yT = bigv(0, KC * NT * 2, BF16, "p (c n) -> p c n", n=NT)
        CC = buf("carryC")
        zero(carryC[:, :, :], CC)
        wn = "L%d_win" % l
        for j in range(NTI):
            norm_tile(xtile(src_ap, j), src_bufs[j], (l, "mn"))
            items = []
            state = {}
            for c in range(KC):
                def mk(which, c=c):
                    def f(sl, slb):
                        ps, psb = newps()
                        w = sl[:, 0:KC * P].rearrange("p (k m) -> p k m", m=P)
                        mm(ps[:, :], psb, [(None, w[:, kc, :], hT[:, kc, :]) for kc in range(KC)], (slb, buf("hT")))
                        state[which] = (ps, psb)
                        if which == "u":
                            (pb, pbb), (pc, pcb), (pu, pub) = state["b"], state["c"], state["u"]
                            cs, csb = newscr()
                            act(cs[:, 0:NT], pc[:, :], AF.Copy, csb, (pcb,))
                            ce, ceb = newscr()
                            tt(ce[:, 2:2 + NT], cs[:, 0:NT], pu[:, :], ALU.mult, ceb, (csb, pub))
                            t, tb = newscr()
                            conv3(ce, ceb, carryC, CC, c, "cw", l, KC, None, t[:, 0:NT], tb, j)
                            tt(yT[:, c, :], t[:, 0:NT], pb[:, :], ALU.mult, BIG, (tb, pbb))
                    return f
                items.append((wn, c, mk("b")))
                items.append((wn, KC + c, mk("c")))
                items.append((wn, 2 * KC + c, mk("u")))
            run_items(items)
            out_proj(l, "wo", yT, BIG, j, src_ap, src_bufs, dst_ap, dst_bufs)

    _ao = [max(2 * KC * NT * 2, KC * NT * 4)]

    def acarve(nbytes, dt, pat=None, **kw):
        a = bigv(_ao[0], nbytes, dt, pat, **kw)
        _ao[0] += nbytes
        return a
    kT = acarve(HKV * 5 * P * 2, BF16, "p (g n) -> p g n", n=5 * P)
    Vt = acarve(5 * HKV * 64 * 2, BF16, "p (b g d) -> p b g d", g=HKV, d=64)
    cosT = acarve(NT * 4, F32)
    sinT = acarve(NT * 4, F32)
    Eb = [acarve(NT * 2, BF16) for i in range(3)]
    assert _ao[0] <= BIGB, _ao[0]
    eb_i = [0]
    bvs = sb("bvs", [P, HKV * 64], F32)
    esink = sb("esink", [P, KC], F32)

    def rope_tables():
        CB, SBf = buf("cosd"), buf("sind")
        invf = smc("invf")
        C1 = 6.28125
        C2 = 2 * math.pi - 6.28125
        LIM = 3.1415925
        for j in range(NTI):
            pi_, pib = newscr()
            pint = pi_[:, 0:NT].bitcast(I32)
            dma_in(pint, posr[:, j * NT:(j + 1) * NT], pib, key="pos")
            ang, angb = newscr()
            cp(ang[:, 0:NT], pint, angb, (pib,))
            ts(ang[:, 0:NT], ang[:, 0:NT], invf, None, ALU.mult, None, angb, (angb, buf("sm")))
            ts(pint, ang[:, 0:NT], 1.0 / (2 * math.pi), None, ALU.mult, None, pib, (angb,))
            kf, kfb = newscr()
            cp(kf[:, 0:NT], pint, kfb, (pib,))
            stt(ang[:, 0:NT], kf[:, 0:NT], -C1, ang[:, 0:NT], ALU.mult, ALU.add, angb, (kfb, angb))
            stt(ang[:, 0:NT], kf[:, 0:NT], -C2, ang[:, 0:NT], ALU.mult, ALU.add, angb, (kfb, angb))
            ts(ang[:, 0:NT], ang[:, 0:NT], LIM, -LIM, ALU.min, ALU.max, angb, (angb,))
            sn, snb = newscr()
            act(sn[:, 0:NT], ang[:, 0:NT], AF.Sin, snb, (angb,))
            E("pool", lambda e, a=sn, j=j: e.dma_start(out=sind[:, j * NT:(j + 1) * NT], in_=a[:, 0:NT]),
              reads=(snb,), writes=(SBf,), dma="sst")
            stt(kf[:, 0:NT], ang[:, 0:NT], -1.0, ang[:, 0:NT], ALU.mult, ALU.max, kfb, (angb,))
            act(kf[:, 0:NT], kf[:, 0:NT], AF.Sin, kfb, (kfb, buf("sm")), bias=smc("hpi"), scale=-1.0)
            E("pool", lambda e, a=kf, j=j: e.dma_start(out=cosd[:, j * NT:(j + 1) * NT], in_=a[:, 0:NT]),
              reads=(kfb,), writes=(CB,), dma="cst")

    def attn_phase(l, ai, src_ap, src_bufs, dst_ap, dst_bufs):
        qT = bigv(0, KC * NT * 2, BF16, "p (c n) -> p c n", n=NT)
        oT = bigv(KC * NT * 2, KC * NT * 2, BF16, "p (c n) -> p c n", n=NT)
        KT, VT, CS, SN, ES, BV = buf("kT"), buf("Vt"), buf("cosT"), buf("sinT"), buf("esink"), buf("bvs")
        dma_in(bvs[:, :], bvd[:, ai * HKV * 64:(ai + 1) * HKV * 64], BV)
        act(esink[:, :], smc((l, "snk"), 0, KC), AF.Exp, ES, (buf("sm"),))
        pre = "L%d_" % l
        for j in range(NTI):
            norm_tile(xtile(src_ap, j), src_bufs[j], (l, "mn"))
            dma_in(cosT[:, :], cosd[:, j * NT:(j + 1) * NT], CS, ibufs=(buf("cosd"),))
            dma_in(sinT[:, :], sind[:, j * NT:(j + 1) * NT], SN, ibufs=(buf("sind"),))

            def qk_post(ps, psb, bias, gain, out_ap, outb):
                r1, r1b = newscr()
                act(r1[:, 0:NT], ps[:, :], AF.Identity, r1b, (psb, buf("sm")), bias=bias)
                sq, sqb = newscr()
                act(sq[:, 0:NT], r1[:, 0:NT], AF.Square, sqb, (r1b,))
                p2, p2b = newps()
                mm(p2[:, :], p2b, [(None, bd64, sq[:, 0:NT])], (sqb, buf("c32")))
                rsqrt(sq[:, 0:NT], p2[:, :], 1.0 / 64, sqb, p2b)
                stt(r1[:, 0:NT], r1[:, 0:NT], gain, sq[:, 0:NT], ALU.mult, ALU.mult, r1b, (r1b, sqb, buf("sm")))
                p3, p3b = newps()
                mm(p3[:, :], p3b, [(None, rotm, r1[:, 0:NT])], (r1b, buf("c32")))
                tt(sq[:, 0:NT], p3[:, :], sinT[:, :], ALU.mult, sqb, (p3b, SN))
                tt(r1[:, 0:NT], r1[:, 0:NT], cosT[:, :], ALU.mult, r1b, (r1b, CS))
                tt(out_ap, r1[:, 0:NT], sq[:, 0:NT], ALU.add, outb, (r1b, sqb))

            items = []
            for c in range(KC):
                def fq(sl, slb, c=c):
                    ps, psb = newps()
                    w = sl[:, 0:KC * P].rearrange("p (k m) -> p k m", m=P)
                    mm(ps[:, :], psb, [(None, w[:, kc, :], hT[:, kc, :]) for kc in range(KC)], (slb, buf("hT")))
                    qk_post(ps, psb, smc((l, "bqk"), c), smc((l, "qg")), qT[:, c, :], BIG)
                items.append((pre + "wq", c, fq))
            for g in range(HKV):
                def fk(sl, slb, g=g):
                    ps, psb = newps()
                    w = sl[:, 0:KC * P].rearrange("p (k m) -> p k m", m=P)
                    mm(ps[:, :], psb, [(None, w[:, kc, :], hT[:, kc, :]) for kc in range(KC)], (slb, buf("hT")))
                    qk_post(ps, psb, smc((l, "bqk"), KC + g), smc((l, "kg")), kT[:, g, P:5 * P], KT)
                items.append((pre + "wk", g, fk))
            vstate = {}
            for tb in range(NB):
                for kg in range(cfg.NKG):
                    def fv(sl, slb, tb=tb, kg=kg):
                        if kg == 0:
                            vstate["p"] = newps()
                        ps, psb = vstate["p"]
                        w = sl[:, 0:cfg.VG * HKV * 64].rearrange("p (k m) -> p k m", m=HKV * 64)
                        mm(ps[:, 0:HKV * 64], psb,
                           [(None, hT[:, kg * cfg.VG + k, tb * P:(tb + 1) * P], w[:, k, :]) for k in range(cfg.VG)],
                           (slb, buf("hT")), start=(kg == 0), stop=(kg == cfg.NKG - 1), relax=(kg > 0))
                        if kg == cfg.NKG - 1:
                            tt(Vt[:, 1 + tb, :, :], ps[:, 0:HKV * 64].rearrange("p (g d) -> p g d", d=64),
                               bvs[:, :].rearrange("p (g d) -> p g d", d=64), ALU.add, VT, (psb, BV))
                    items.append((pre + "wv", kg, fv))
            run_items(items)
            for b in range(NB):
                gb = j * NB + b
                for g in range(HKV):
                    pO, pOb = newps()
                    pD, pDb = newps()
                    kbs = [1] if gb == 0 else [0, 1]
                    for e_ in range(2):
                        pr_ = slice(e_ * 64, (e_ + 1) * 64)
                        for ki, kb in enumerate(kbs):
                            kblk = b + kb
                            pS, pSb = newps()
                            mm(pS[:, 0:CPG * P].rearrange("p (c n) -> p c n", n=P), pSb,
                               [(None, kT[pr_, g, kblk * P:(kblk + 1) * P], qT[pr_, g * CPG:(g + 1) * CPG, b * P:(b + 1) * P])],
                               (KT, BIG))
                            i3 = eb_i[0] % 3
                            eb_i[0] += 1
                            Ei, Eib = Eb[i3], buf("Eb%d" % i3)
                            act(Ei[:, 0:CPG * P], pS[:, 0:CPG * P], AF.Exp, Eib, (pSb,), scale=0.125)
                            msk = maskp if kb == 0 else maskc
                            tt(Ei[:, 0:CPG * P], Ei[:, 0:CPG * P], msk[:, 0:CPG * P], ALU.mult, Eib, (Eib, buf("cb")), eng="pool")
                            prs = []
                            for cj in range(CPG):
                                prs.append(((pO[pr_, cj * P:(cj + 1) * P], Vt[:, kblk, g, :], Ei[:, cj * P:(cj + 1) * P]), cj))
                            def fn(e, prs=prs, ki=ki, nk=len(kbs), pD=pD, pr_=pr_, Ei=Ei):
                                ins = None
                                for (o, lh, r), cj in prs:
                                    st_ = (ki == 0 and cj == 0)
                                    e.matmul(o, lh, r, start=st_, stop=(ki == nk - 1), skip_group_check=True)
                                    ins = e.matmul(pD[pr_, cj * P:(cj + 1) * P], ones64, r, start=st_, stop=(ki == nk - 1),
                                                   skip_group_check=True)
                                return ins
                            E("pe", fn, reads=(Eib, VT, buf("cb")), writes=(pOb, pDb), relax=not (e_ == 0 and ki == 0))
                    dn, dnb = newscr()
                    tt(dn[:, 0:CPG * P].rearrange("p (c n) -> p c n", n=P), pD[:, 0:CPG * P].rearrange("p (c n) -> p c n", n=P),
                       esink[:, g * CPG:(g + 1) * CPG].rearrange("p (c o) -> p c o", o=1).to_broadcast([P, CPG, P]),
                       ALU.add, dnb, (pDb, ES))
                    E("dve", lambda e, dn=dn: e.reciprocal(out=dn[:, 0:CPG * P], in_=dn[:, 0:CPG * P]), reads=(dnb,), writes=(dnb,))
                    tt(oT[:, g * CPG:(g + 1) * CPG, b * P:(b + 1) * P], pO[:, 0:CPG * P].rearrange("p (c n) -> p c n", n=P),
                       dn[:, 0:CPG * P].rearrange("p (c n) -> p c n", n=P), ALU.mult, BIG, (pOb, dnb))
            cp(kT[:, :, 0:P], kT[:, :, 4 * P:5 * P], KT, (KT,), eng="pool")
            cp(Vt[:, 0, :, :], Vt[:, NB, :, :], VT, (VT,), eng="pool")
            out_proj(l, "wo", oT, BIG, j, src_ap, src_bufs, dst_ap, dst_bufs, bias_key="bo")

    lbt = sb("lbt", [P, 3 * KC], F32)

    def hgrn_phase(l, src_ap, src_bufs, dst_ap, dst_bufs):
        NCH = NT // 64
        off = [0]

        def carve(nbytes, dt, pat=None, **kw):
            a = bigv(off[0], nbytes, dt, pat, **kw)
            off[0] += nbytes
            return a
        yT = carve(KC * NT * 2, BF16, "p (c n) -> p c n", n=NT)
        Sbc = carve((NCH + 1) * P * 2, BF16, "p (c v) -> p c v", v=P)
        f32s = [carve(NT * 4, F32) for _ in range(9)]
        sgt = carve(NT * 4, F32)
        b16s = [carve(NT * 2, BF16) for _ in range(7)]
        sml = carve(4 * NCH * 4, F32, "p (a c) -> p a c", c=NCH)
        off[0] = max(off[0], KC * NT * 4)
        Sst = carve(KC * P * 4, F32, "p (h v) -> p h v", v=P)
        Sb0 = carve(KC * P * 2, BF16, "p (h v) -> p h v", v=P)
        assert off[0] <= BIGB, off[0]
        HB = {n: buf("hg_" + n) for n in ("yT", "S", "Sb0", "Sbc", "sml", "f0", "f1", "f2", "f3", "f4", "f5", "f6", "f7", "f8",
                                         "b0", "b1", "b2", "b3", "b4", "b5", "b6", "sgt")}
        LB = buf("lbt")
        ex, exb = newscr()
        act(ex[:, 0:L * KC], smc((l, "lbs"), 0, L * KC), AF.Exp, exb, (buf("sm"),))
        exv = ex[:, 0:L * KC].rearrange("p (l c) -> p l c", c=KC)
        tot, totb = newscr()
        cp(tot[:, 0:KC], exv[:, 0, :], totb, (exb,))
        for jj in range(1, L):
            tt(tot[:, 0:KC], tot[:, 0:KC], exv[:, jj, :], ALU.add, totb, (totb, exb))
        E("dve", lambda e: e.reciprocal(out=tot[:, 0:KC], in_=tot[:, 0:KC]), reads=(totb,), writes=(totb,))
        E("dve", lambda e: e.memset(lbt[:, 0:KC], 0.0), writes=(LB,))
        for jj in range(1, l + 1):
            tt(lbt[:, 0:KC], lbt[:, 0:KC], exv[:, jj, :], ALU.add, LB, (LB, exb))
        tt(lbt[:, 0:KC], lbt[:, 0:KC], tot[:, 0:KC], ALU.mult, LB, (LB, totb))
        ts(lbt[:, KC:2 * KC], lbt[:, 0:KC], -1.0, 1.0, ALU.mult, ALU.add, LB, (LB,))
        ts(lbt[:, 2 * KC:3 * KC], lbt[:, 0:KC], -1.0, None, ALU.add, None, LB, (LB,))
        barrier()
        zero(Sst[:, :, :], HB["S"], eng="dve")
        zero(Sb0[:, :, :], HB["Sb0"], eng="dve")
        wn = "L%d_win" % l
        (qs, sg, fg, kk, bb, Ex, Ei, qbf, ki) = f32s
        (qb, qe, ke, kl, klT, vtok, Am) = b16s
        FB = [HB["f%d" % i] for i in range(9)]
        BB = [HB["b%d" % i] for i in range(7)]
        for j in range(NTI):
            barrier()
            norm_tile(xtile(src_ap, j), src_bufs[j], (l, "mn"))
            barrier()
            items = []
            hst = {}
            for hd in range(KC):
                def mk(which, hd=hd):
                    def f(sl, slb):
                        ps, psb = newps()
                        w = sl[:, 0:KC * P].rearrange("p (k m) -> p k m", m=P)
                        if which != "i":
                            mm(ps[:, :], psb, [(None, w[:, kc, :], hT[:, kc, :]) for kc in range(KC)], (slb, buf("hT")))
                        else:
                            mm_groups(psb, [[(ps[:, tb * P:(tb + 1) * P], hT[:, kc, tb * P:(tb + 1) * P], w[:, kc, :]) for kc in range(KC)]
                                            for tb in range(NB)], (slb, buf("hT")))
                        hst[which] = (ps, psb)
                        if which == "q":
                            act(qs, ps[:, :], AF.Silu, FB[0], (psb,))
                        elif which == "f":
                            act(sg, ps[:, :], AF.Sigmoid, FB[1], (psb,))
                        elif which == "g":
                            act(sgt, ps[:, :], AF.Silu, HB["sgt"], (psb,))
                        elif which == "i":
                            cp(vtok, ps[:, :], BB[5], (psb,), eng="act" if False else "dve")
                            head_core(hd)
                    return f
                items.append((wn, hd, mk("q")))
                items.append((wn, KC + hd, mk("f")))
                items.append((wn, 3 * KC + hd, mk("g")))
                items.append((wn, 2 * KC + hd, mk("i")))

            def head_core(hd):
                lb, om, nom = lbt[:, hd:hd + 1], lbt[:, KC + hd:KC + hd + 1], lbt[:, 2 * KC + hd:2 * KC + hd + 1]
                if HGSTOP <= 1:
                    return
                ts(fg, sg, om, lb, ALU.mult, ALU.add, FB[2], (FB[1], LB))
                act(fg, fg, AF.Ln, FB[2], (FB[2],))
                ts(kk, sg, nom, om, ALU.mult, ALU.add, FB[3], (FB[1], LB))
                E("dve", lambda e: e.tensor_tensor_scan(out=bb, data0=resetm, data1=fg, initial=0.0, op0=ALU.mult, op1=ALU.add),
                  reads=(FB[2], buf("cb")), writes=(FB[4],))
                if HGSTOP <= 2:
                    return
                act(Ex, bb, AF.Exp, FB[5], (FB[4],))
                act(Ei, bb, AF.Exp, FB[6], (FB[4],), scale=-1.0)
                bv = bb.rearrange("p (c t) -> p c t", t=64)
                act(sml[:, 0, :], bv[:, :, 63], AF.Exp, HB["sml"], (FB[4],))
                act(sml[:, 1, :], bv[:, :, 31], AF.Exp, HB["sml"], (FB[4], HB["sml"]), scale=-1.0)
                act(sml[:, 2, :], bv[:, :, 31], AF.Exp, HB["sml"], (FB[4], HB["sml"]))
                if HGSTOP <= 3:
                    return
                tt(qbf, qs, Ex, ALU.mult, FB[7], (FB[0], FB[5]))
                cp(qb, qbf, BB[0], (FB[7],), eng="pool")
                v3 = lambda a: a.rearrange("p (c t) -> p c t", t=64)
                bc = lambda i: sml[:, i, :].rearrange("p (c o) -> p c o", o=1).to_broadcast([P, NCH, 64])
                tt(v3(qe), v3(qbf), bc(1), ALU.mult, BB[1], (FB[7], HB["sml"]))
                tt(ki, kk, Ei, ALU.mult, FB[8], (FB[3], FB[6]))
                tt(v3(ke), v3(ki), bc(2), ALU.mult, BB[2], (FB[8], HB["sml"]))
                tt(v3(kl), v3(ki), bc(0), ALU.mult, BB[3], (FB[8], HB["sml"]))
                if HGSTOP <= 4:
                    return
                pT, pTb = newps()
                pTv = pT[:, :].bitcast(BF16)
                def ftr(e):
                    ins = None
                    for tb in range(NB):
                        ins = e.transpose(pTv[:, tb * P:(tb + 1) * P], kl[:, tb * P:(tb + 1) * P], ident)
                    return ins
                E("pe", ftr, reads=(BB[3], buf("cb")), writes=(pTb,))
                act(klT, pTv[:, 0:NT], AF.Copy, BB[4], (pTb,))
                if HGSTOP <= 5:
                    return
                pU0, pU0b = newps()
                pU1, pU1b = newps()
                grp = []
                for c8 in range(NCH):
                    tb, hf = c8 // 2, c8 % 2
                    rows = slice(hf * 64, (hf + 1) * 64)
                    pU = pU0 if hf == 0 else pU1
                    grp.append((pU[:, tb * P:(tb + 1) * P], klT[rows, tb * P:(tb + 1) * P], vtok[rows, tb * P:(tb + 1) * P]))

                def fpu(e, grp=grp):
                    ins = None
                    for o, lh, r in grp:
                        ins = e.matmul(o, lh, r, start=True, stop=True, skip_group_check=True)
                    return ins
                E("pe", fpu, reads=(BB[4], BB[5]), writes=(pU0b, pU1b))
                if HGSUB <= 1:
                    return
                cp(Sbc[:, 0, :], Sb0[:, hd, :], HB["Sbc"], (HB["Sb0"],))
                if HGSUB <= 2:
                    return
                for c8 in range(NCH if HGSUB > 3 else 1):
                    pU, pUb = (pU0, pU0b) if c8 % 2 == 0 else (pU1, pU1b)
                    cc = c8 // 2
                    stt(Sst[:, hd, :], Sst[:, hd, :], sml[:, 0, c8:c8 + 1], pU[:, cc * P:(cc + 1) * P], ALU.mult, ALU.add,
                        HB["S"], (HB["S"], HB["sml"], pUb))
                    if HGSUB <= 3:
                        return
                    cp(Sbc[:, c8 + 1, :], Sst[:, hd, :], HB["Sbc"], (HB["S"],))
                cp(Sb0[:, hd, :], Sbc[:, NCH, :], HB["Sb0"], (HB["Sbc"],))
                if HGSTOP <= 6:
                    return
                pA, pAb = newps()
                mm_groups(pAb, [[(pA[:, tb * P:(tb + 1) * P], ke[:, tb * P:(tb + 1) * P], qe[:, tb * P:(tb + 1) * P])] for tb in range(NB)],
                          (BB[2], BB[1]))
                tt(Am, pA[:, :], hmask, ALU.mult, BB[6], (pAb, buf("cb")))
                pY, pYb = newps()
                grp = []
                for c8 in range(NCH):
                    tb = c8 // 2
                    cs_ = slice(c8 * 64, (c8 + 1) * 64)
                    grp.append([(pY[:, cs_], vtok[:, tb * P:(tb + 1) * P], Am[:, cs_]), (pY[:, cs_], Sbc[:, c8, :], qb[:, cs_])])
                mm_groups(pYb, grp, (BB[5], BB[6], HB["Sbc"], BB[0]))
                if HGSTOP <= 7:
                    return
                act(qs, pY[:, :], AF.Square, FB[0], (pYb,))
                pN, pNb = newps()
                mm(pN[:, :], pNb, [(None, ones32, qs)], (FB[0], buf("c32")))
                rsqrt(sg, pN[:, :], 1.0 / P, FB[1], pNb)
                stt(kk, pY[:, :], smc((l, "og")), sg, ALU.mult, ALU.mult, FB[3], (pYb, FB[1], buf("sm")))
                tt(yT[:, hd, :], kk, sgt, ALU.mult, HB["yT"], (FB[3], HB["sgt"]))
            run_items(items)
            out_proj(l, "wo", yT, HB["yT"], j, src_ap, src_bufs, dst_ap, dst_bufs)

    def barrier():
        allb = list(B.values())
        for e in ("pe", "act", "dve", "pool", "sp"):
            waits = [(k, v) for k, v in pr.cnt.items() if pr.waited[e].get(k, 0) < v]
            for k, v in waits:
                pr.waited[e][k] = v
            pr.ops[e].append((waits, None, None, 0))

    XI = [buf("xi%d" % j) for j in range(NTI)]
    XM = [buf("xm%d" % j) for j in range(NTI)]
    XN = [buf("xn%d" % j) for j in range(NTI)]
    YO = [buf("yo%d" % j) for j in range(NTI)]
    prep_layer(0)
    if 0 in cfg.kinds:
        rope_tables()
    ai = 0
    for l in range(L):
        if l + 1 < L:
            prep_layer(l + 1)
        src_ap, src_b = (xin, XI) if l == 0 else (xn, XN)
        dst_ap, dst_b = (yout, YO) if l == L - 1 else (xn, XN)
        k = cfg.kinds[l]
        barrier()
        dma_in(sm[:, cfg.NG:cfg.NSM], smd[:, cfg.NG + l * cfg.LMAX:cfg.NG + (l + 1) * cfg.LMAX], buf("sm"))
        if k == 0:
            attn_phase(l, ai, src_ap, src_b, xm, XM)
            ai += 1
        elif k == 1:
            hgrn_phase(l, src_ap, src_b, xm, XM)
        else:
            conv_phase(l, src_ap, src_b, xm, XM)
        barrier()
        ffn_phase(l, xm, XM, dst_ap, dst_b)
    barrier()
    pr.emit(nc)
    st.close()
    return nc


def _blk(W, cw):
    K, Fd = W.shape
    return np.ascontiguousarray(W.reshape(K // P, P, Fd // cw, cw).transpose(2, 1, 0, 3)).reshape(Fd // cw, P, (K // P) * cw)


def _pp(v):
    return np.ascontiguousarray(v.reshape(-1, P).T)


def host_inputs(cfg, inp):
    KC, FC, FH, HKV, L = cfg.KC, cfg.FC, cfg.FH, cfg.HKV, cfg.L
    D = cfg.D
    smfull = np.zeros((P, cfg.NSMD), np.float32)
    shared = {}
    cnt = {0: 0, 1: 0, 2: 0}
    bv = []
    for l, k in enumerate(cfg.kinds):
        j = cnt[k]
        cnt[k] += 1
        c = cfg.col
        sm = np.zeros((P, cfg.NSM), np.float32)
        sm[:, c[(l, "mn")]:c[(l, "mn")] + KC] = _pp(inp["mixer_norm"][l])
        sm[:, c[(l, "fn")]:c[(l, "fn")] + KC] = _pp(inp["mlp_norm"][l])
        for t in range(3):
            sm[:, c[(l, "mcw")] + t * FC:c[(l, "mcw")] + (t + 1) * FC] = _pp(inp["mlp_conv_w"][l, t])
        sm[:, c[(l, "mcb")]:c[(l, "mcb")] + FC] = _pp(inp["mlp_conv_b"][l])
        wup = inp["mlp_w_up"][l]
        shared["L%d_wup" % l] = _blk(wup, P)
        wd = inp["mlp_w_down"][l]
        wdp = np.zeros((2 * FH * P, D), np.float32)
        wdp[:FH * P] = wd[:FH * P]
        wdp[FH * P:FH * P + (FC - FH) * P] = wd[FH * P:]
        b0 = _blk(wdp[:FH * P], P)
        b1 = _blk(wdp[FH * P:], P)
        shared["L%d_wdn" % l] = np.ascontiguousarray(np.stack([b0, b1], 1).reshape(2 * KC, P, FH * P))
        if k == 0:
            wqkv, bq = inp["attn_w_qkv"][j], inp["attn_b_qkv"][j]
            shared["L%d_wq" % l] = _blk(wqkv[:, :D], P)
            wk = wqkv[:, D:D + HKV * 64].reshape(D, HKV, 64)
            shared["L%d_wk" % l] = _blk(np.concatenate([wk, wk], 2).reshape(D, HKV * P), P)
            wv = wqkv[:, D + HKV * 64:]
            shared["L%d_wv" % l] = np.ascontiguousarray(
                wv.reshape(cfg.NKG, cfg.VG, P, HKV * 64).transpose(0, 2, 1, 3)).reshape(cfg.NKG, P, cfg.VG * HKV * 64)
            shared["L%d_wo" % l] = _blk(inp["attn_w_o"][j], P)
            o = c[(l, "bqk")]
            sm[:, o:o + KC] = _pp(bq[:D])
            bk = bq[D:D + HKV * 64].reshape(HKV, 64)
            sm[:, o + KC:o + KC + HKV] = np.concatenate([bk, bk], 1).T
            sm[:, c[(l, "qg")]] = np.tile(inp["attn_q_norm"][j], 2)
            sm[:, c[(l, "kg")]] = np.tile(inp["attn_k_norm"][j], 2)
            sm[:, c[(l, "snk")]:c[(l, "snk")] + KC] = np.repeat(inp["attn_sinks"][j].reshape(KC, 2), 64, axis=1).T
            sm[:, c[(l, "bo")]:c[(l, "bo")] + KC] = _pp(inp["attn_b_o"][j])
            bv.append(np.tile(bq[D + HKV * 64:][None, :], (P, 1)))
        elif k == 1:
            shared["L%d_win" % l] = _blk(inp["hgrn_w_in"][j], P)
            shared["L%d_wo" % l] = _blk(inp["hgrn_w_o"][j], P)
            o = c[(l, "lbs")]
            for jj in range(L):
                sm[:, o + jj * KC:o + (jj + 1) * KC] = _pp(inp["hgrn_lower_bounds"][jj])
            sm[:, c[(l, "og")]] = inp["hgrn_out_norm"][j]
        else:
            shared["L%d_win" % l] = _blk(inp["conv_w_in"][j], P)
            shared["L%d_wo" % l] = _blk(inp["conv_w_out"][j], P)
            for t in range(3):
                sm[:, c[(l, "cw")] + t * KC:c[(l, "cw")] + (t + 1) * KC] = _pp(inp["conv_w"][j, t])
        smfull[:, cfg.NG + l * cfg.LMAX:cfg.NG + (l + 1) * cfg.LMAX] = sm[:, cfg.NG:]
    half = 32
    sm = smfull
    invf = (10000.0 ** (-np.arange(half, dtype=np.float32) / half)).astype(np.float32)
    sm[:, cfg.col["invf"]] = np.tile(invf, 4)
    sm[:, cfg.col["eps"]] = EPS
    sm[:, cfg.col["hpi"]] = math.pi / 2
    shared["sm"] = sm
    NT = cfg.NT
    ones = np.ones((P, P), np.float32)
    bd = np.zeros((P, P), np.float32)
    bd[:64, :64] = 1
    bd[64:, 64:] = 1
    rot = np.zeros((P, P), np.float32)
    for h in range(2):
        for d in range(32):
            rot[h * 64 + d + 32, h * 64 + d] = -1.0
            rot[h * 64 + d, h * 64 + d + 32] = 1.0
    shared["cf32"] = np.concatenate([ones, bd, rot], 1)
    kk, qq = np.arange(P)[:, None], np.arange(P)[None, :]
    mc = (kk <= qq).astype(np.float32)
    mp = (kk > qq).astype(np.float32)
    hm = ((kk <= qq) & ((kk // 64) == (qq // 64))).astype(np.float32)
    rs = np.ones((P, NT), np.float32)
    rs[:, ::64] = 0
    rep = NT // P
    shared["cbf"] = np.concatenate([np.tile(mc, (1, rep)), np.tile(mp, (1, rep)), np.tile(hm, (1, rep)), rs,
                                    np.eye(P, dtype=np.float32), np.ones((P, 64), np.float32)], 1)
    shared["bvrep"] = np.concatenate(bv, 1) if bv else np.zeros((P, HKV * 64), np.float32)
    return shared


def run(cfg, inp, trace=False):
    shared = host_inputs(cfg, inp)
    nc = build_nc(cfg)
    x = inp["x"]
    Bn = x.shape[0]
    in_maps = []
    for b in range(Bn):
        m = dict(shared)
        m["xT"] = np.ascontiguousarray(x[b].T)
        m["posr"] = np.ascontiguousarray(np.tile(inp["positions"][b].astype(np.int32)[None, :], (P, 1)))
        in_maps.append(m)
    res = run_bass_kernel_spmd(nc, in_maps, core_ids=list(range(Bn)), **({"trace": True} if trace else {}))
    out = np.stack([np.ascontiguousarray(res.results[b]["yT"].T) for b in range(Bn)], 0)
    return out.astype(np.float32), res


def kernel(**inputs):
    inp = {k: np.asarray(v) for k, v in inputs.items()}
    cfg = Cfg()
    out, _ = run(cfg, inp)
    return out
```
